# Optimizing a Trainium2 kernel written in Bass

```python
import math
import jax, jax.numpy as jnp
from jax import lax
import numpy as np

D_MODEL = 1024
BATCH = 8
SEQ = 4096
DEPTH = 1

HGRN_EXPAND = 128
HGRN_HEADS = D_MODEL // HGRN_EXPAND
HGRN_DK = HGRN_EXPAND
HGRN_DV = D_MODEL // HGRN_HEADS
HGRN_K_WIDTH = HGRN_HEADS * HGRN_DK
HGRN_V_WIDTH = HGRN_HEADS * HGRN_DV
CHUNK = 64
CONV_CH = D_MODEL
CONV_K = 31
D_FF = 2816
FFN_RESIDUAL = 0.5
N_MOD = 9
EPS = 1e-6
IN_SIZES = (HGRN_K_WIDTH, HGRN_K_WIDTH, HGRN_V_WIDTH, HGRN_V_WIDTH, 2 * CONV_CH, D_MODEL, D_MODEL)
IN_WIDTH = sum(IN_SIZES)
IN_SPLITS = tuple(int(s) for s in np.cumsum(IN_SIZES)[:-1])

kernel_name = "hybrid_hgrn2_conformer_macaron_adaln"


def rms_norm(x, g):
    xf = x.astype(jnp.float32)
    y = xf * lax.rsqrt(jnp.mean(xf * xf, axis=-1, keepdims=True) + EPS)
    return (y * g.astype(jnp.float32)).astype(x.dtype)


def layer_norm(x, g, b):
    xf = x.astype(jnp.float32)
    mu = jnp.mean(xf, axis=-1, keepdims=True)
    xc = xf - mu
    y = xc * lax.rsqrt(jnp.mean(xc * xc, axis=-1, keepdims=True) + EPS)
    return (y * g.astype(jnp.float32) + b.astype(jnp.float32)).astype(x.dtype)


def swiglu(h, w_in, w_out):
    a, b = jnp.split(h @ w_in, 2, axis=-1)
    return (jax.nn.silu(a) * b) @ w_out


def hgrn2_chunked(q, k, v, logf):
    B, S, H, DK = q.shape
    DV = v.shape[-1]
    nc = S // CHUNK

    def to_chunks(t):
        return t.reshape(B, nc, CHUNK, H, t.shape[-1]).transpose(1, 0, 3, 2, 4)

    causal = jnp.tril(jnp.ones((CHUNK, CHUNK), dtype=bool))

    def step(state, inp):
        qc, kc, vc, gc = inp
        b = jnp.cumsum(gc, axis=-2)
        diff = b[:, :, :, None, :] - b[:, :, None, :, :]
        decay = jnp.exp(jnp.where(causal[:, :, None], diff, -jnp.inf))
        att = jnp.einsum('bhtd,bhsd,bhtsd->bhts', qc, kc, decay)
        o_intra = jnp.einsum('bhts,bhsv->bhtv', att, vc)
        o_inter = jnp.einsum('bhtd,bhdv->bhtv', qc * jnp.exp(b), state)
        b_last = b[:, :, -1, :]
        k_dec = kc * jnp.exp(b_last[:, :, None, :] - b)
        new_state = jnp.exp(b_last)[..., None] * state + jnp.einsum('bhsd,bhsv->bhdv', k_dec, vc)
        return new_state, o_intra + o_inter

    state0 = jnp.zeros((B, H, DK, DV), jnp.float32)
    _, o = lax.scan(step, state0, (to_chunks(q), to_chunks(k), to_chunks(v), to_chunks(logf)))
    return o.transpose(1, 0, 3, 2, 4).reshape(B, S, H, DV)


def causal_depthwise_conv(u, w, b):
    y = lax.conv_general_dilated(
        u, w[:, None, :].astype(u.dtype), window_strides=(1,), padding=[(CONV_K - 1, 0)],
        dimension_numbers=('NWC', 'WIO', 'NWC'), feature_group_count=u.shape[-1])
    return y + b


def token_mixer(h, lb, w_in, hgrn_g, hgrn_w_o, conv_w, conv_b, conv_ln_g, conv_ln_b, conv_w_o, w_out):
    B, S, _ = h.shape
    f32 = jnp.float32
    q, f, i, og, u, ga, gb = jnp.split(h @ w_in, IN_SPLITS, axis=-1)
    q = (jax.nn.silu(q.astype(f32)) * (HGRN_DK ** -0.5)).reshape(B, S, HGRN_HEADS, HGRN_DK)
    fg = lb + (1.0 - lb) * jax.nn.sigmoid(f.astype(f32))
    logf = jnp.log(fg).reshape(B, S, HGRN_HEADS, HGRN_DK)
    k = (1.0 - fg).reshape(B, S, HGRN_HEADS, HGRN_DK)
    v = i.astype(f32).reshape(B, S, HGRN_HEADS, HGRN_DV)
    o = hgrn2_chunked(q, k, v, logf)
    o = o * lax.rsqrt(jnp.mean(o * o, axis=-1, keepdims=True) + EPS)
    o = o * hgrn_g.astype(f32).reshape(HGRN_HEADS, HGRN_DV)
    o = (o.reshape(B, S, HGRN_V_WIDTH) * jax.nn.silu(og.astype(f32))).astype(h.dtype)
    y_a = o @ hgrn_w_o
    ua, ub = jnp.split(u, 2, axis=-1)
    u = ua * jax.nn.sigmoid(ub)
    u = causal_depthwise_conv(u, conv_w, conv_b)
    u = jax.nn.silu(layer_norm(u, conv_ln_g, conv_ln_b))
    y_b = u @ conv_w_o
    merged = jax.nn.sigmoid(ga) * y_a + jax.nn.sigmoid(gb) * y_b
    return merged @ w_out


def setup_inputs(seed: int = 0) -> dict:
    key = jax.random.key(seed)
    ks = jax.random.split(key, 24)
    D, L = D_MODEL, DEPTH
    nrm = lambda k, shape, fan_in: jax.random.normal(k, shape, jnp.float32) * (fan_in ** -0.5)
    gain = lambda k, shape: 1.0 + 0.02 * jax.random.normal(k, shape, jnp.float32)
    small = lambda k, shape: 0.02 * jax.random.normal(k, shape, jnp.float32)
    return {
        "x": jax.random.normal(ks[0], (BATCH, SEQ, D), jnp.float32),
        "c": jax.random.normal(ks[1], (BATCH, D), jnp.float32),
        "ada_w": nrm(ks[2], (L, D, N_MOD * D), D),
        "ada_b": small(ks[3], (L, N_MOD * D)),
        "norm_ffn1": gain(ks[4], (L, D)),
        "ffn1_w_in": nrm(ks[5], (L, D, 2 * D_FF), D),
        "ffn1_w_out": nrm(ks[6], (L, D_FF, D), D_FF),
        "norm_mix": gain(ks[7], (L, D)),
        "mix_w_in": nrm(ks[8], (L, D, IN_WIDTH), D),
        "hgrn_lb": 0.1 * jax.random.normal(ks[9], (L + 1, HGRN_K_WIDTH), jnp.float32),
        "hgrn_g": gain(ks[10], (L, HGRN_V_WIDTH)),
        "hgrn_w_o": nrm(ks[11], (L, HGRN_V_WIDTH, D), HGRN_V_WIDTH),
        "conv_w": nrm(ks[12], (L, CONV_K, CONV_CH), CONV_K),
        "conv_b": small(ks[13], (L, CONV_CH)),
        "conv_ln_g": gain(ks[14], (L, CONV_CH)),
        "conv_ln_b": small(ks[15], (L, CONV_CH)),
        "conv_w_o": nrm(ks[16], (L, CONV_CH, D), CONV_CH),
        "mix_w_out": nrm(ks[17], (L, D, D), D),
        "norm_ffn2": gain(ks[18], (L, D)),
        "ffn2_w_in": nrm(ks[19], (L, D, 2 * D_FF), D),
        "ffn2_w_out": nrm(ks[20], (L, D_FF, D), D_FF),
        "norm_final": gain(ks[21], (D,)),
    }


def reference(x, c, ada_w, ada_b, norm_ffn1, ffn1_w_in, ffn1_w_out, norm_mix, mix_w_in,
              hgrn_lb, hgrn_g, hgrn_w_o, conv_w, conv_b, conv_ln_g, conv_ln_b, conv_w_o,
              mix_w_out, norm_ffn2, ffn2_w_in, ffn2_w_out, norm_final):
    B = x.shape[0]
    lb_all = jnp.cumsum(jax.nn.softmax(hgrn_lb.astype(jnp.float32), axis=0), axis=0)
    cs = jax.nn.silu(c)
    for l in range(DEPTH):
        mod = (cs @ ada_w[l] + ada_b[l]).reshape(B, N_MOD, D_MODEL)
        sh1, sc1, g1, sh2, sc2, g2, sh3, sc3, g3 = [mod[:, j, None, :] for j in range(N_MOD)]
        h = rms_norm(x, norm_ffn1[l]) * (1.0 + sc1) + sh1
        x = x + FFN_RESIDUAL * g1 * swiglu(h, ffn1_w_in[l], ffn1_w_out[l])
        h = rms_norm(x, norm_mix[l]) * (1.0 + sc2) + sh2
        x = x + g2 * token_mixer(h, lb_all[l], mix_w_in[l], hgrn_g[l], hgrn_w_o[l], conv_w[l],
                                 conv_b[l], conv_ln_g[l], conv_ln_b[l], conv_w_o[l], mix_w_out[l])
        h = rms_norm(x, norm_ffn2[l]) * (1.0 + sc3) + sh3
        x = x + FFN_RESIDUAL * g3 * swiglu(h, ffn2_w_in[l], ffn2_w_out[l])
    return rms_norm(x, norm_final)
```

```python
import numpy as np
import concourse.bass as bass
import concourse.mybir as mybir
from concourse.bass_utils import run_bass_kernel_spmd

F32 = mybir.dt.float32
BF16 = mybir.dt.bfloat16
AF = mybir.ActivationFunctionType
ALU = mybir.AluOpType

D = 1024
DFF = 2816
NJ = DFF // 128
T = 512
SEQ = 4096
NH = 8
CONV_K = 31
HALO = CONV_K - 1
EPS = 1e-6
NSLOT = 5
SLOT_E = 4096


class Buf:
    __slots__ = ("name", "w", "r", "sem", "cnt")

    def __init__(self, name):
        self.name = name
        self.w = None
        self.r = {}
        self.sem = None
        self.cnt = 0


class Eng:
    def __init__(self, fw, name, eng, is_pe=False):
        self.fw = fw
        self.name = name
        self.eng = eng
        self.is_pe = is_pe
        self.seen = {}
        self.queue = []
        self.sem = fw.new_sem(name)
        self.cnt = 0

    def new_epoch(self):
        self.sem = self.fw.new_sem(self.name)
        self.cnt = 0


class FW:
    def __init__(self, nc):
        self.nc = nc
        self.sems = []
        self.pe = Eng(self, "pe", nc.tensor, is_pe=True)
        self.act = Eng(self, "act", nc.scalar)
        self.dve = Eng(self, "dve", nc.vector)
        self.pool = Eng(self, "pool", nc.gpsimd)
        self.sp = Eng(self, "sp", nc.sync)
        self.engines = [self.pe, self.act, self.dve, self.pool, self.sp]
        self.nbuf = 0
        self.ninst = 0

    def new_sem(self, name):
        h = self.nc.alloc_semaphore(f"s{len(self.sems)}_{name}")
        self.sems.append(h)
        return len(self.sems) - 1

    def buf(self, name=None):
        self.nbuf += 1
        return Buf(name or f"b{self.nbuf}")

    def new_epoch(self):
        for e in self.engines:
            if e.cnt > 0:
                e.new_epoch()

    def op(self, E, fn, reads=(), writes=(), dma=None):
        deps = {}

        def add(tok):
            if tok is None:
                return
            s, v = tok
            if deps.get(s, 0) < v:
                deps[s] = v

        for b in reads:
            add(b.w)
        for b in writes:
            add(b.w)
            for tok in b.r.values():
                add(tok)
        waits = []
        for s, v in deps.items():
            if E.is_pe and dma is None and s == E.sem:
                continue
            if E.seen.get(s, 0) >= v:
                continue
            E.seen[s] = v
            waits.append((s, v))
        if dma is not None:
            if dma.sem is None:
                dma.sem = self.new_sem("d_" + dma.name)
            dma.cnt += 16
            tok = (dma.sem, dma.cnt)
            inc = (dma.sem, 16)
        else:
            E.cnt += 1
            tok = (E.sem, E.cnt)
            inc = (E.sem, 1)
        E.queue.append((waits, fn, inc))
        self.ninst += 1 + len(waits)
        for b in reads:
            b.r[tok[0]] = tok
        for b in writes:
            b.w = tok
            b.r = {}
        return tok

    def wait_bufs(self, E, bufs):
        deps = {}
        for b in bufs:
            toks = list(b.r.values()) + ([b.w] if b.w else [])
            for s, v in toks:
                if deps.get(s, 0) < v:
                    deps[s] = v
        waits = [(s, v) for s, v in deps.items() if E.seen.get(s, 0) < v]
        for s, v in waits:
            E.seen[s] = v
        E.queue.append((waits, None, None))

    def emit(self):
        nc = self.nc
        sems = self.sems
        with nc.Block() as block:
            def run(E):
                def body(eng):
                    for waits, fn, inc in E.queue:
                        for s, v in waits:
                            eng.wait_ge(sems[s], v)
                        if fn is not None:
                            ins = fn(eng)
                            ins.then_inc(sems[inc[0]], inc[1])
                return body
            block.tensor(run(self.pe))
            block.scalar(run(self.act))
            block.vector(run(self.dve))
            block.gpsimd(run(self.pool))
            block.sync(run(self.sp))


class TB:
    def __init__(self, t, b):
        self.t = t
        self.b = b


def _blockify(W, blocks):
    din, dout = W.shape
    kc = din // 128
    Wv = W.reshape(kc, 128, dout // 128, 128)
    arr = Wv[:, :, blocks, :]
    return np.ascontiguousarray(arr.transpose(1, 2, 0, 3)).reshape(128, -1)


def _weight_groups():
    g = []
    for f in ("ffn1", "ffn2"):
        pass
    def ffn(name):
        out = []
        for i in range(NJ // 2):
            out.append((name + "_w_in", [2 * i, NJ + 2 * i, 2 * i + 1, NJ + 2 * i + 1], 8))
        for dc in range(8):
            out.append((name + "_w_out", [dc], NJ))
        return out
    g += ffn("ffn1")
    for h in range(NH):
        g.append(("mix_w_in", [h, 8 + h, 16 + h, 24 + h], 8))
    for cg in range(4):
        g.append(("mix_w_in", [32 + 2 * cg, 40 + 2 * cg, 32 + 2 * cg + 1, 40 + 2 * cg + 1], 8))
    for half in range(2):
        g.append(("hgrn_w_o", [4 * half + i for i in range(4)], 8))
        g.append(("conv_w_o", [4 * half + i for i in range(4)], 8))
        for gg in (2 * half, 2 * half + 1):
            g.append(("mix_w_in", [48 + 2 * gg, 56 + 2 * gg, 48 + 2 * gg + 1, 56 + 2 * gg + 1], 8))
    for half in range(2):
        g.append(("mix_w_out", [4 * half + i for i in range(4)], 8))
    g += ffn("ffn2")
    return g


WGROUPS = _weight_groups()
WG_E = [len(b) * kc * 128 for (_, b, kc) in WGROUPS]
WG_OFF = [0]
for _e in WG_E:
    WG_OFF.append(WG_OFF[-1] + 128 * _e)
WALL_N = WG_OFF[-1]

ADA_NB = 72
PC = {}
_c = 0
for _n, _w in (("c", 8), ("ada_b", 72), ("nf1", 8), ("nmix", 8), ("nf2", 8), ("nfin", 8), ("lb0", 8), ("lb1", 8),
               ("hg", 8), ("cb", 8), ("lng", 8), ("lnb", 8), ("cw", 8 * CONV_K)):
    PC[_n] = (_c, _c + _w)
    _c += _w
NPC = _c


def _fm(v):
    return np.ascontiguousarray(v.reshape(-1, 128).T)


def build(NT=SEQ // T, debug=False, stage=9):
    nc = bass.Bass("TRN2", target_bir_lowering=False)
    fw = FW(nc)
    pe, act, dve, pool, sp = fw.pe, fw.act, fw.dve, fw.pool, fw.sp
    S = NT * T

    xT_d = nc.dram_tensor("xT", [D, S], F32, kind="ExternalInput").ap()
    wall_d = nc.dram_tensor("wall", [WALL_N], F32, kind="ExternalInput").ap()
    ada_d = nc.dram_tensor("ada", [ADA_NB // 2, 128, 2 * 8 * 128], F32, kind="ExternalInput").ap()
    pp_d = nc.dram_tensor("pp", [128, NPC], F32, kind="ExternalInput").ap()
    outT_d = nc.dram_tensor("outT", [D, S], F32, kind="ExternalOutput").ap()
    wscr_d = nc.dram_tensor("wscr", [WALL_N], BF16, kind="Internal").ap()
    dbg_outs = {}

    def sb(name, shape, dt):
        return TB(nc.alloc_sbuf_tensor("sb_" + name, shape, dt), fw.buf(name))

    wslots = [sb(f"wslot{i}", [128, SLOT_E], BF16) for i in range(NSLOT)]
    xs = [sb(f"xs{i}", [128, 8, T], F32) for i in range(2)]
    hT = sb("hT", [128, 8, T], BF16)
    mrg = sb("mrg", [128, 8, T], BF16)
    rs = sb("rs", [128, T], F32)
    ntmp = [sb(f"ntmp{i}", [128, T], F32) for i in range(2)]
    arena = sb("arena", [128, NJ * T // 2], F32)
    gT = arena.t[:, :].bitcast(BF16).rearrange("p (j t) -> p j t", t=T)
    acc = arena.t[:, 0:8 * T].rearrange("p (c t) -> p c t", t=T)
    sa = [sb(f"sa{i}", [128, T], F32) for i in range(2)]
    ogT = sb("ogT", [128, 8, T], BF16)
    u2T = sb("u2T", [128, 8, T], BF16)
    ub = [sb(f"ub{i}", [128, HALO + T], F32) for i in range(2)]
    halo = sb("halo", [128, 8, HALO], F32)
    Sf = sb("Sf", [128, NH, 128], F32)
    Sb0 = sb("Sb0", [128, NH, 128], BF16)
    pp = sb("pp", [128, NPC], F32)
    dv = sb("dv", [128, 160], F32)
    modv = sb("modv", [128, ADA_NB], F32)
    ones = sb("ones", [128, 128], BF16)
    ident = sb("ident", [128, 128], BF16)
    identf = sb("identf", [128, 128], F32)
    smask = sb("smask", [128, T], F32)
    mask2 = sb("mask2", [128, 128], F32)
    adast = [TB(arena.t[:, 2048 * i:2048 * (i + 1)], fw.buf(f"adast{i}")) for i in range(2)]
    NHS = 2
    hs = []
    for i in range(NHS):
        hs.append(dict(
            B1=sb(f"hB1_{i}", [128, T], F32), B2=sb(f"hB2_{i}", [128, T], F32), B3=sb(f"hB3_{i}", [128, T], F32),
            sq=sb(f"hsq_{i}", [128, T], F32), qt=sb(f"hqt_{i}", [128, T], BF16), kt=sb(f"hkt_{i}", [128, T], BF16),
            sog=sb(f"hsog_{i}", [128, T], BF16), Vb=sb(f"hVb_{i}", [128, 4, 128], BF16),
            ktok=sb(f"hktok_{i}", [128, 4, 128], BF16), ktokO=sb(f"hktokO_{i}", [128, 4, 128], BF16), Am=sb(f"hAm_{i}", [128, 4, 128], BF16),
            Sball=sb(f"hSb_{i}", [128, 9, 128], BF16), T1=[sb(f"hT1_{i}_{k}", [128, 128], F32) for k in range(2)],
        ))
        hs[-1]["osq"] = hs[-1]["kt"]
        hs[-1]["ot"] = hs[-1]["B1"]
    mtmp = [sb(f"mtmp{i}", [128, T], F32) for i in range(4)]

    NPS = 7
    psr = [TB(nc.alloc_psum_tensor(f"ps{i}", [128, T], F32), fw.buf(f"ps{i}")) for i in range(NPS)]
    pstr = TB(nc.alloc_psum_tensor("pstr", [128, 4, 128], BF16), fw.buf("pstr"))
    ps_i = [0]

    def nps():
        p = psr[ps_i[0] % NPS]
        ps_i[0] += 1
        return p

    def bl(xs_):
        return [x.b if isinstance(x, TB) else x for x in xs_]

    def ACT(out, in_, func, R, W, **kw):
        fw.op(act, lambda e: e.activation(out=out, in_=in_, func=func, **kw), reads=bl(R), writes=bl(W))

    def TT(E, out, in0, in1, op, R, W):
        fw.op(E, lambda e: e.tensor_tensor(out=out, in0=in0, in1=in1, op=op), reads=bl(R), writes=bl(W))

    def TS(E, out, in0, s1, s2, op0, op1, R, W):
        if s2 is None:
            fw.op(E, lambda e: e.tensor_scalar(out=out, in0=in0, scalar1=s1, scalar2=None, op0=op0), reads=bl(R), writes=bl(W))
        else:
            fw.op(E, lambda e: e.tensor_scalar(out=out, in0=in0, scalar1=s1, scalar2=s2, op0=op0, op1=op1), reads=bl(R), writes=bl(W))

    def STT(E, out, in0, scalar, in1, op0, op1, R, W):
        fw.op(E, lambda e: e.scalar_tensor_tensor(out=out, in0=in0, scalar=scalar, in1=in1, op0=op0, op1=op1), reads=bl(R), writes=bl(W))

    def MM(out, lhsT, rhs, start, stop, R, W):
        fw.op(pe, lambda e: e.matmul(out, lhsT=lhsT, rhs=rhs, start=start, stop=stop), reads=bl(R), writes=bl(W))

    def CP(E, out, in_, R, W):
        fw.op(E, lambda e: e.tensor_copy(out=out, in_=in_), reads=bl(R), writes=bl(W))

    def DMA(E, out, in_, R, W, owner):
        fw.op(E, lambda e: e.dma_start(out=out, in_=in_), reads=bl(R), writes=bl(W), dma=owner.b if isinstance(owner, TB) else owner)

    def MEMSET(E, ap, val, W):
        fw.op(E, lambda e: e.memset(ap, val), writes=bl(W))

    def RSQRT(otb, out, itb, in_, epsname):
        ACT(out, in_, AF.Ln, [itb, dv], [otb], bias=dvc(epsname, 0))
        ACT(out, out, AF.Exp, [otb], [otb], scale=-0.5)

    def dbg(name, tb, ap, shape, dt):
        if not debug:
            return
        d = nc.dram_tensor("dbg_" + name, list(shape), dt, kind="ExternalOutput").ap()
        dbg_outs[name] = fw.buf("dbg_" + name)
        DMA(sp, d, ap, [tb], [dbg_outs[name]], dbg_outs[name])

    def col(name, i=None):
        a, b_ = PC[name]
        if i is None:
            return pp.t[:, a:b_]
        return pp.t[:, a + i:a + i + 1]

    DVC = {}
    _o = [0]

    def dvcol(name, w=8):
        DVC[name] = (_o[0], _o[0] + w)
        _o[0] += w

    for n_ in ("cs", "gm1", "gt1", "gm2", "gm3", "gt3", "nfin", "lb", "oml", "noml", "hg", "lbd", "epsD", "epsH", "eps1", "rmE", "rmO"):
        dvcol(n_)

    def dvc(name, i=None):
        a, b_ = DVC[name]
        if i is None:
            return dv.t[:, a:b_]
        return dv.t[:, a + i:a + i + 1]

    def modc(k, i=None):
        if i is None:
            return modv.t[:, 8 * k:8 * k + 8]
        return modv.t[:, 8 * k + i:8 * k + i + 1]

    DMA(sp, pp.t[:, :], pp_d[:, :], [], [pp], pp)
    wscr_b = fw.buf("wscr")
    wscr_sems = []
    CH = 128 * SLOT_E * 2
    off = 0
    ci = 0
    cast_bufs = []
    import os as _os
    _nocast = bool(int(_os.environ.get('NOCAST', '0')))
    while off < WALL_N and not (_nocast and ci >= 0 + int(_os.environ.get('NCAST', '0'))):
        n = min(CH, WALL_N - off)
        assert n % 2048 == 0
        src = wall_d[off:off + n].rearrange("(r c) -> r c", c=2048)
        dst = wscr_d[off:off + n].rearrange("(r c) -> r c", c=2048)
        cb_ = fw.buf(f"cast{ci % 4}") if ci < 4 else cast_bufs[ci % 4]
        if ci < 4:
            cast_bufs.append(cb_)
        fw.op(pool, (lambda s_, d_: (lambda e: e.dma_start(out=d_, in_=s_)))(src, dst), reads=[], writes=[cb_], dma=cb_)
        off += n
        ci += 1

    MEMSET(pool, ones.t[:, :], 1.0, [ones])
    MEMSET(pool, identf.t[:, :], 1.0, [identf])
    fw.op(pool, lambda e: e.affine_select(out=identf.t[:, :], in_=identf.t[:, :], pattern=[[-1, 128]], compare_op=ALU.is_equal,
                                          fill=0.0, base=0, channel_multiplier=1), reads=[identf.b], writes=[identf.b])
    CP(dve, ident.t[:, :], identf.t[:, :], [identf], [ident])
    MEMSET(pool, mask2.t[:, :], 1.0, [mask2])
    fw.op(pool, lambda e: e.affine_select(out=mask2.t[:, :], in_=mask2.t[:, :], pattern=[[1, 128]], compare_op=ALU.is_ge,
                                          fill=0.0, base=0, channel_multiplier=-1), reads=[mask2.b], writes=[mask2.b])
    MEMSET(pool, mask2.t[0:64, 64:128], 0.0, [mask2])
    MEMSET(pool, dvc("rmE")[0:64, :], 1.0, [dv])
    MEMSET(pool, dvc("rmE")[64:128, :], 0.0, [dv])
    MEMSET(pool, dvc("rmO")[0:64, :], 0.0, [dv])
    MEMSET(pool, dvc("rmO")[64:128, :], 1.0, [dv])
    MEMSET(pool, smask.t[:, :], 1.0, [smask])
    MEMSET(pool, smask.t[:, :].rearrange("p (c t) -> p c t", t=64)[:, :, 0:1], 0.0, [smask])
    MEMSET(pool, Sf.t[:, :, :], 0.0, [Sf])
    MEMSET(pool, Sb0.t[:, :, :], 0.0, [Sb0])
    MEMSET(pool, halo.t[:, :, :], 0.0, [halo])

    MEMSET(pool, dvc("epsD"), float(D * EPS), [dv])
    MEMSET(pool, dvc("epsH"), float(128 * EPS), [dv])
    MEMSET(pool, dvc("eps1"), float(EPS), [dv])
    ACT(dvc("cs"), col("c"), AF.Silu, [pp], [dv])
    psmod = nps()
    for g2 in range(ADA_NB // 2):
        st = adast[g2 % 2]
        DMA(sp, st.t, ada_d[g2], [], [st], st)
        stv = st.t.rearrange("p (b k c) -> p b k c", b=2, k=8)
        for b_ in range(2):
            j = 2 * g2 + b_
            for kc in range(8):
                MM(psmod.t[:, j:j + 1], stv[:, b_, kc, :], dvc("cs", kc), kc == 0, kc == 7, [st, dv], [psmod])
    TT(dve, modv.t[:, :], psmod.t[:, 0:ADA_NB], col("ada_b"), ALU.add, [psmod, pp], [modv])
    for (gm, nf, sc) in (("gm1", "nf1", 1), ("gm2", "nmix", 4), ("gm3", "nf2", 7)):
        TS(dve, dvc(gm), modc(sc), 1.0, 32.0, ALU.add, ALU.mult, [modv], [dv])
        TT(dve, dvc(gm), dvc(gm), col(nf), ALU.mult, [dv, pp], [dv])
    TS(dve, dvc("gt1"), modc(2), 0.5, None, ALU.mult, None, [modv], [dv])
    TS(dve, dvc("gt3"), modc(8), 0.5, None, ALU.mult, None, [modv], [dv])
    TS(dve, dvc("nfin"), col("nfin"), 32.0, None, ALU.mult, None, [pp], [dv])
    TT(dve, dvc("lbd"), col("lb0"), col("lb1"), ALU.subtract, [pp], [dv])
    ACT(dvc("lb"), dvc("lbd"), AF.Sigmoid, [dv], [dv])
    TS(dve, dvc("oml"), dvc("lb"), -1.0, 1.0, ALU.mult, ALU.add, [dv], [dv])
    TS(dve, dvc("noml"), dvc("oml"), -1.0, None, ALU.mult, None, [dv], [dv])
    TS(dve, dvc("hg"), col("hg"), float(np.sqrt(128.0)), None, ALU.mult, None, [pp], [dv])
    if debug:
        dbg("modv", modv, modv.t[:, :], [128, ADA_NB], F32)

    NG = len(WGROUPS)
    wstate = dict(next_load=0, next_use=0)
    total_groups = NT * NG

    def issue_load():
        n = wstate["next_load"]
        if n >= total_groups:
            return
        g = n % NG
        slot = wslots[n % NSLOT]
        E = WG_E[g]
        src = wscr_d[WG_OFF[g]:WG_OFF[g] + 128 * E].rearrange("(p e) -> p e", p=128)
        DMA(sp, slot.t[:, 0:E], src, cast_bufs, [slot], slot)
        wstate["next_load"] = n + 1

    def next_w(expect_key):
        n = wstate["next_use"]
        g = n % NG
        key, blocks, kc = WGROUPS[g]
        assert key == expect_key, (key, expect_key)
        slot = wslots[n % NSLOT]
        wstate["next_use"] = n + 1
        v = slot.t[:, 0:WG_E[g]].rearrange("p (b k c) -> p b k c", b=len(blocks), k=kc)
        return slot, v

    def done_w():
        issue_load()

    def rms_stats(x):
        ACT(mrg.t[:, :, :], x.t[:, :, :], AF.Square, [x], [mrg])
        p = nps()
        for dc in range(8):
            MM(p.t[:, :], ones.t[:, :], mrg.t[:, dc, :], dc == 0, dc == 7, [ones, mrg], [p])
        RSQRT(rs, rs.t[:, :], p, p.t[:, :], "epsD")

    def norm_mod(x, gm, shk):
        rms_stats(x)
        for dc in range(8):
            tmp = ntmp[dc % 2]
            TT(dve, tmp.t[:, :], x.t[:, dc, :], rs.t[:, :], ALU.mult, [x, rs], [tmp])
            ACT(hT.t[:, dc, :], tmp.t[:, :], AF.Identity, [tmp, dv, modv], [hT], scale=dvc(gm, dc), bias=modc(shk, dc))

    def ffn(x, name, gm, shk, gt):
        norm_mod(x, gm, shk)
        for g in range(NJ // 2):
            slot, wv = next_w(name + "_w_in")
            for jj in range(2):
                j = 2 * g + jj
                pa = nps()
                for kc in range(8):
                    MM(pa.t[:, :], wv[:, 2 * jj, kc, :], hT.t[:, kc, :], kc == 0, kc == 7, [slot, hT], [pa])
                pb = nps()
                for kc in range(8):
                    MM(pb.t[:, :], wv[:, 2 * jj + 1, kc, :], hT.t[:, kc, :], kc == 0, kc == 7, [slot, hT], [pb])
                s_ = sa[j % 2]
                ACT(s_.t[:, :], pa.t[:, :], AF.Silu, [pa], [s_])
                TT(dve, gT[:, j, :], pb.t[:, :], s_.t[:, :], ALU.mult, [pb, s_], [arena])
            done_w()
        for dc in range(8):
            slot, wv = next_w(name + "_w_out")
            po = nps()
            for j in range(NJ):
                MM(po.t[:, :], wv[:, 0, j, :], gT[:, j, :], j == 0, j == NJ - 1, [slot, arena], [po])
            done_w()
            STT(dve, x.t[:, dc, :], po.t[:, :], dvc(gt, dc), x.t[:, dc, :], ALU.mult, ALU.add, [po, dv, x], [x])

    def hgrn_head_front(h, H):
        slot, wv = next_w("mix_w_in")
        pq = nps()
        for kc in range(8):
            MM(pq.t[:, :], wv[:, 0, kc, :], hT.t[:, kc, :], kc == 0, kc == 7, [slot, hT], [pq])
        pf = nps()
        for kc in range(8):
            MM(pf.t[:, :], wv[:, 1, kc, :], hT.t[:, kc, :], kc == 0, kc == 7, [slot, hT], [pf])
        pg = nps()
        for kc in range(8):
            MM(pg.t[:, :], wv[:, 3, kc, :], hT.t[:, kc, :], kc == 0, kc == 7, [slot, hT], [pg])
        pv = nps()
        pvv = pv.t[:, :].rearrange("p (b c) -> p b c", c=128)
        for tb in range(4):
            for kc in range(8):
                MM(pvv[:, tb, :], hT.t[:, kc, tb * 128:(tb + 1) * 128], wv[:, 2, kc, :], kc == 0, kc == 7, [slot, hT], [pv])
        done_w()
        B1, B2, B3 = H["B1"], H["B2"], H["B3"]
        ACT(H["sq"].t[:, :], pq.t[:, :], AF.Silu, [pq], [H["sq"]])
        ACT(H["sog"].t[:, :], pg.t[:, :], AF.Silu, [pg], [H["sog"]])
        ACT(B1.t[:, :], pf.t[:, :], AF.Sigmoid, [pf], [B1])
        CP(dve, H["Vb"].t[:, :, :], pvv, [pv], [H["Vb"]])
        ACT(B2.t[:, :], B1.t[:, :], AF.Ln, [B1, dv], [B2], scale=dvc("oml", h), bias=dvc("lb", h))
        fw.op(dve, lambda e: e.tensor_tensor_scan(out=B3.t[:, :], data0=smask.t[:, :], data1=B2.t[:, :], initial=0.0,
                                                  op0=ALU.mult, op1=ALU.add), reads=[smask.b, B2.b], writes=[B3.b])
        TS(dve, B1.t[:, :], B1.t[:, :], dvc("noml", h), dvc("oml", h), ALU.mult, ALU.add, [B1, dv], [B1])
        ACT(B2.t[:, :], B3.t[:, :], AF.Exp, [B3], [B2])
        ACT(B3.t[:, :], B3.t[:, :], AF.Exp, [B3], [B3], scale=-1.0)
        STT(dve, H["qt"].t[:, :], H["sq"].t[:, :], float(128.0 ** -0.5), B2.t[:, :], ALU.mult, ALU.mult, [H["sq"], B2], [H["qt"]])
        TT(dve, H["kt"].t[:, :], B1.t[:, :], B3.t[:, :], ALU.mult, [B1, B3], [H["kt"]])
        for tb in range(4):
            fw.op(pe, (lambda tb_: (lambda e: e.transpose(out=pstr.t[:, tb_, :], in_=H["kt"].t[:, tb_ * 128:(tb_ + 1) * 128],
                                                          identity=ident.t[:, :])))(tb),
                  reads=[H["kt"].b, ident.b], writes=[pstr.b])
        TS(dve, H["ktok"].t[:, :, :], pstr.t[:, :, :], dvc("rmE", 0), None, ALU.mult, None, [pstr, dv], [H["ktok"]])
        TS(dve, H["ktokO"].t[:, :, :], pstr.t[:, :, :], dvc("rmO", 0), None, ALU.mult, None, [pstr, dv], [H["ktokO"]])
        pa = nps()
        pav = pa.t[:, :].rearrange("p (b c) -> p b c", c=128)
        for tb in range(4):
            sl = slice(tb * 128, (tb + 1) * 128)
            MM(pav[:, tb, :], H["kt"].t[:, sl], H["qt"].t[:, sl], True, True, [H["kt"], H["qt"]], [pa])
        TT(dve, H["Am"].t[:, :, :], pav, mask2.t[:, :].unsqueeze(1).to_broadcast([128, 4, 128]), ALU.mult, [pa, mask2], [H["Am"]])
        pus = [nps(), nps()]
        for c in range(8):
            tb, pb_ = c // 2, (c % 2) * 64
            pu = pus[c // 4]
            kk = H["ktok"] if c % 2 == 0 else H["ktokO"]
            MM(pu.t[:, (c % 4) * 128:(c % 4 + 1) * 128], kk.t[:, tb, :], H["Vb"].t[:, tb, :],
               True, True, [kk, H["Vb"]], [pu])
        Sball = H["Sball"]
        for c in range(8):
            pu = pus[c // 4]
            T1 = H["T1"][c % 2]
            TT(dve, T1.t[:, :], pu.t[:, (c % 4) * 128:(c % 4 + 1) * 128], Sf.t[:, h, :], ALU.add, [pu, Sf], [T1])
            ebl = B2.t[:, c * 64 + 63:c * 64 + 64]
            ACT(Sf.t[:, h, :], T1.t[:, :], AF.Identity, [T1, B2], [Sf], scale=ebl)
            TS(dve, Sball.t[:, c + 1, :], T1.t[:, :], ebl, None, ALU.mult, None, [T1, B2], [Sball])
        return pus

    _alt = [0]

    def act_or_dve():
        return dve

    def hgrn_head_back(h, H, pus):
        Sball = H["Sball"]
        po = nps()
        for c in range(8):
            tb, pb_ = c // 2, (c % 2) * 64
            cs_ = slice(c * 64, (c + 1) * 64)
            if c == 0:
                MM(po.t[:, cs_], Sb0.t[:, h, :], H["qt"].t[:, cs_], True, False, [Sb0, H["qt"]], [po])
            else:
                MM(po.t[:, cs_], Sball.t[:, c, :], H["qt"].t[:, cs_], True, False, [Sball, H["qt"]], [po])
            MM(po.t[:, cs_], H["Vb"].t[:, tb, :], H["Am"].t[:, tb, pb_:pb_ + 64], False, True,
               [H["Vb"], H["Am"]], [po])
        CP(dve, Sb0.t[:, h, :], Sball.t[:, 8, :], [Sball], [Sb0])
        ACT(H["osq"].t[:, :], po.t[:, :], AF.Square, [po], [H["osq"]])
        pn = nps()
        MM(pn.t[:, :], ones.t[:, :], H["osq"].t[:, :], True, True, [ones, H["osq"]], [pn])
        RSQRT(H["ot"], H["ot"].t[:, :], pn, pn.t[:, :], "epsH")
        TT(dve, H["ot"].t[:, :], po.t[:, :], H["ot"].t[:, :], ALU.mult, [po, H["ot"]], [H["ot"]])
        STT(dve, ogT.t[:, h, :], H["ot"].t[:, :], dvc("hg", h), H["sog"].t[:, :], ALU.mult, ALU.mult, [H["ot"], dv, H["sog"]], [ogT])

    def conv_branch():
        for cg in range(4):
            slot, wv = next_w("mix_w_in")
            for cc in range(2):
                cbk = 2 * cg + cc
                pa = nps()
                for kc in range(8):
                    MM(pa.t[:, :], wv[:, 2 * cc, kc, :], hT.t[:, kc, :], kc == 0, kc == 7, [slot, hT], [pa])
                pb = nps()
                for kc in range(8):
                    MM(pb.t[:, :], wv[:, 2 * cc + 1, kc, :], hT.t[:, kc, :], kc == 0, kc == 7, [slot, hT], [pb])
                s_ = sa[cbk % 2]
                U = ub[cbk % 2]
                ACT(s_.t[:, :], pb.t[:, :], AF.Sigmoid, [pb], [s_])
                ACT(U.t[:, 0:HALO], halo.t[:, cbk, :], AF.Copy, [halo], [U])
                TT(dve, U.t[:, HALO:HALO + T], pa.t[:, :], s_.t[:, :], ALU.mult, [pa, s_], [U])
                ACT(halo.t[:, cbk, :], U.t[:, T:T + HALO], AF.Copy, [U], [halo])
                a0, _ = PC["cw"]
                wcol = lambda j: pp.t[:, a0 + cbk * CONV_K + j:a0 + cbk * CONV_K + j + 1]
                TS(dve, acc[:, cbk, :], U.t[:, 0:T], wcol(0), col("cb", cbk), ALU.mult, ALU.add, [U, pp], [arena])
                for j in range(1, CONV_K):
                    STT(dve, acc[:, cbk, :], U.t[:, j:j + T], wcol(j), acc[:, cbk, :], ALU.mult, ALU.add, [U, pp, arena], [arena])
            done_w()
        ACT(u2T.t[:, :, :], acc, AF.Copy, [arena], [u2T])
        ACT(mrg.t[:, :, :], acc, AF.Square, [arena], [mrg])
        pm = nps()
        for cbk in range(8):
            MM(pm.t[:, :], ones.t[:, :], u2T.t[:, cbk, :], cbk == 0, cbk == 7, [ones, u2T], [pm])
        pq2 = nps()
        for cbk in range(8):
            MM(pq2.t[:, :], ones.t[:, :], mrg.t[:, cbk, :], cbk == 0, cbk == 7, [ones, mrg], [pq2])
        M1, M2 = mtmp[0], mtmp[1]
        TS(dve, M1.t[:, :], pm.t[:, :], 1.0 / D, None, ALU.mult, None, [pm], [M1])
        TT(dve, M2.t[:, :], M1.t[:, :], M1.t[:, :], ALU.mult, [M1], [M2])
        STT(dve, M2.t[:, :], pq2.t[:, :], 1.0 / D, M2.t[:, :], ALU.mult, ALU.subtract, [pq2, M2], [M2])
        RSQRT(M2, M2.t[:, :], M2, M2.t[:, :], "eps1")
        STT(dve, M1.t[:, :], M1.t[:, :], -1.0, M2.t[:, :], ALU.mult, ALU.mult, [M1, M2], [M1])
        for cbk in range(8):
            t1 = mtmp[2 + cbk % 2]
            TT(dve, t1.t[:, :], acc[:, cbk, :], M2.t[:, :], ALU.mult, [arena, M2], [t1])
            TT(dve, t1.t[:, :], t1.t[:, :], M1.t[:, :], ALU.add, [t1, M1], [t1])
            ACT(u2T.t[:, cbk, :], t1.t[:, :], AF.Silu, [t1, pp], [u2T], scale=col("lng", cbk), bias=col("lnb", cbk))

    def merge_and_out(x):
        for half in range(2):
            sA, wA = next_w("hgrn_w_o")
            sB, wB = next_w("conv_w_o")
            sG0, wG0 = next_w("mix_w_in")
            sG1, wG1 = next_w("mix_w_in")
            for i in range(4):
                dc = 4 * half + i
                sG, wG = (sG0, wG0) if i < 2 else (sG1, wG1)
                ii = i % 2
                pya = nps()
                for kc in range(8):
                    MM(pya.t[:, :], wA[:, i, kc, :], ogT.t[:, kc, :], kc == 0, kc == 7, [sA, ogT], [pya])
                pyb = nps()
                for kc in range(8):
                    MM(pyb.t[:, :], wB[:, i, kc, :], u2T.t[:, kc, :], kc == 0, kc == 7, [sB, u2T], [pyb])
                pga = nps()
                for kc in range(8):
                    MM(pga.t[:, :], wG[:, 2 * ii, kc, :], hT.t[:, kc, :], kc == 0, kc == 7, [sG, hT], [pga])
                pgb = nps()
                for kc in range(8):
                    MM(pgb.t[:, :], wG[:, 2 * ii + 1, kc, :], hT.t[:, kc, :], kc == 0, kc == 7, [sG, hT], [pgb])
                g1, g2_, m1, m2 = mtmp[0], mtmp[1], mtmp[2], mtmp[3]
                ACT(g1.t[:, :], pga.t[:, :], AF.Sigmoid, [pga], [g1])
                ACT(g2_.t[:, :], pgb.t[:, :], AF.Sigmoid, [pgb], [g2_])
                TT(dve, m1.t[:, :], pya.t[:, :], g1.t[:, :], ALU.mult, [pya, g1], [m1])
                TT(dve, m2.t[:, :], pyb.t[:, :], g2_.t[:, :], ALU.mult, [pyb, g2_], [m2])
                TT(dve, mrg.t[:, dc, :], m1.t[:, :], m2.t[:, :], ALU.add, [m1, m2], [mrg])
            for _ in range(4):
                done_w()
        for half in range(2):
            slot, wv = next_w("mix_w_out")
            for i in range(4):
                dc = 4 * half + i
                po = nps()
                for kc in range(8):
                    MM(po.t[:, :], wv[:, i, kc, :], mrg.t[:, kc, :], kc == 0, kc == 7, [slot, mrg], [po])
                STT(dve, x.t[:, dc, :], po.t[:, :], modc(5, dc), x.t[:, dc, :], ALU.mult, ALU.add, [po, modv, x], [x])
            done_w()

    def mixer(x, ti):
        norm_mod(x, "gm2", 3)
        prev = None
        for h in range(NH):
            H = hs[h % NHS]
            pus = hgrn_head_front(h, H)
            if prev is not None:
                hgrn_head_back(*prev)
            prev = (h, H, pus)
        hgrn_head_back(*prev)
        if debug and ti == 0:
            dbg("ogT", ogT, ogT.t[:, :, :], [128, 8, T], BF16)
        conv_branch()
        if debug and ti == 0:
            dbg("u2T", u2T, u2T.t[:, :, :], [128, 8, T], BF16)
        merge_and_out(x)

    xv = xT_d.rearrange("(dc p) t -> p dc t", p=128)
    ov = outT_d.rearrange("(dc p) t -> p dc t", p=128)
    if stage >= 1:
        for _ in range(NSLOT):
            issue_load()
    DMA(sp, xs[0].t[:, :, :], xv[:, :, 0:T], [], [xs[0]], xs[0])
    for ti in range(NT if stage >= 1 else 0):
        fw.new_epoch()
        x = xs[ti % 2]
        if ti + 1 < NT:
            xn = xs[(ti + 1) % 2]
            DMA(sp, xn.t[:, :, :], xv[:, :, (ti + 1) * T:(ti + 2) * T], [], [xn], xn)
        ffn(x, "ffn1", "gm1", 0, "gt1")
        if debug and ti == 0:
            dbg("x1", x, x.t[:, :, :], [128, 8, T], F32)
        if stage < 2:
            DMA(sp, ov[:, :, ti * T:(ti + 1) * T], x.t[:, :, :], [x], [], x)
            continue
        mixer(x, ti)
        if debug and ti == 0:
            dbg("x2", x, x.t[:, :, :], [128, 8, T], F32)
        if stage < 3:
            DMA(sp, ov[:, :, ti * T:(ti + 1) * T], x.t[:, :, :], [x], [], x)
            continue
        ffn(x, "ffn2", "gm3", 6, "gt3")
        rms_stats(x)
        for dc in range(8):
            STT(dve, x.t[:, dc, :], x.t[:, dc, :], dvc("nfin", dc), rs.t[:, :], ALU.mult, ALU.mult, [x, dv, rs], [x])
        DMA(sp, ov[:, :, ti * T:(ti + 1) * T], x.t[:, :, :], [x], [], x)
    fw.wait_bufs(sp, [xs[0].b, xs[1].b] + list(dbg_outs.values()) + cast_bufs)
    fw.emit()
    return nc, fw


def prep_shared(inp):
    W = {
        "ffn1_w_in": inp["ffn1_w_in"][0], "ffn1_w_out": inp["ffn1_w_out"][0],
        "ffn2_w_in": inp["ffn2_w_in"][0], "ffn2_w_out": inp["ffn2_w_out"][0],
        "mix_w_in": inp["mix_w_in"][0], "hgrn_w_o": inp["hgrn_w_o"][0],
        "conv_w_o": inp["conv_w_o"][0], "mix_w_out": inp["mix_w_out"][0],
    }
    wall = np.empty(WALL_N, np.float32)
    for g, (key, blocks, kc) in enumerate(WGROUPS):
        wall[WG_OFF[g]:WG_OFF[g + 1]] = _blockify(W[key], blocks).reshape(-1)
    ada = np.stack([_blockify(inp["ada_w"][0], [2 * g, 2 * g + 1]) for g in range(ADA_NB // 2)])
    return wall, ada


def prep_pp(inp, b):
    pp = np.zeros((128, NPC), np.float32)

    def put(name, arr):
        a, b_ = PC[name]
        pp[:, a:b_] = arr

    put("c", _fm(inp["c"][b]))
    put("ada_b", _fm(inp["ada_b"][0]))
    put("nf1", _fm(inp["norm_ffn1"][0]))
    put("nmix", _fm(inp["norm_mix"][0]))
    put("nf2", _fm(inp["norm_ffn2"][0]))
    put("nfin", _fm(inp["norm_final"]))
    put("lb0", _fm(inp["hgrn_lb"][0]))
    put("lb1", _fm(inp["hgrn_lb"][1]))
    put("hg", _fm(inp["hgrn_g"][0]))
    put("cb", _fm(inp["conv_b"][0]))
    put("lng", _fm(inp["conv_ln_g"][0]))
    put("lnb", _fm(inp["conv_ln_b"][0]))
    cw = inp["conv_w"][0]
    put("cw", np.ascontiguousarray(cw.reshape(CONV_K, 8, 128).transpose(2, 1, 0)).reshape(128, 8 * CONV_K))
    return pp


_CACHE = {}


def kernel(**inputs):
    inp = {k: np.asarray(v) for k, v in inputs.items()}
    x = inp["x"]
    B, S, _ = x.shape
    NT = S // T
    if NT not in _CACHE:
        _CACHE[NT] = build(NT)[0]
    nc = _CACHE[NT]
    wall, ada = prep_shared(inp)
    in_maps = []
    for b in range(B):
        in_maps.append({
            "xT": np.ascontiguousarray(x[b].T),
            "wall": wall,
            "ada": ada,
            "pp": prep_pp(inp, b),
        })
    res = run_bass_kernel_spmd(nc, in_maps, core_ids=list(range(B)))
    out = np.stack([np.ascontiguousarray(r["outT"].T) for r in res.results], axis=0)
    return out.astype(np.float32, copy=False)
```

```python
import numpy as np
import concourse.bass as bass
import concourse.mybir as mybir
from concourse.bass_utils import run_bass_kernel_spmd

F32 = mybir.dt.float32
BF16 = mybir.dt.bfloat16
AF = mybir.ActivationFunctionType
ALU = mybir.AluOpType

D = 1024
DFF = 2816
NJ = DFF // 128
T = 512
SEQ = 4096
NH = 8
CONV_K = 31
HALO = CONV_K - 1
EPS = 1e-6
NSLOT = 5
SLOT_E = 4096


class Buf:
    __slots__ = ("name", "w", "r", "sem", "cnt")

    def __init__(self, name):
        self.name = name
        self.w = None
        self.r = {}
        self.sem = None
        self.cnt = 0


class Eng:
    def __init__(self, fw, name, eng, is_pe=False):
        self.fw = fw
        self.name = name
        self.eng = eng
        self.is_pe = is_pe
        self.seen = {}
        self.queue = []
        self.sem = fw.new_sem(name)
        self.cnt = 0

    def new_epoch(self):
        self.sem = self.fw.new_sem(self.name)
        self.cnt = 0


class FW:
    def __init__(self, nc):
        self.nc = nc
        self.sems = []
        self.pe = Eng(self, "pe", nc.tensor, is_pe=True)
        self.act = Eng(self, "act", nc.scalar)
        self.dve = Eng(self, "dve", nc.vector)
        self.pool = Eng(self, "pool", nc.gpsimd)
        self.sp = Eng(self, "sp", nc.sync)
        self.engines = [self.pe, self.act, self.dve, self.pool, self.sp]
        self.nbuf = 0
        self.ninst = 0

    def new_sem(self, name):
        h = self.nc.alloc_semaphore(f"s{len(self.sems)}_{name}")
        self.sems.append(h)
        return len(self.sems) - 1

    def buf(self, name=None):
        self.nbuf += 1
        return Buf(name or f"b{self.nbuf}")

    def new_epoch(self):
        for e in self.engines:
            if e.cnt > 0:
                e.new_epoch()

    def op(self, E, fn, reads=(), writes=(), dma=None):
        deps = {}

        def add(tok):
            if tok is None:
                return
            s, v = tok
            if deps.get(s, 0) < v:
                deps[s] = v

        for b in reads:
            add(b.w)
        for b in writes:
            add(b.w)
            for tok in b.r.values():
                add(tok)
        waits = []
        for s, v in deps.items():
            if E.is_pe and dma is None and s == E.sem:
                continue
            if E.seen.get(s, 0) >= v:
                continue
            E.seen[s] = v
            waits.append((s, v))
        if dma is not None:
            if dma.sem is None:
                dma.sem = self.new_sem("d_" + dma.name)
            dma.cnt += 16
            tok = (dma.sem, dma.cnt)
            inc = (dma.sem, 16)
        else:
            E.cnt += 1
            tok = (E.sem, E.cnt)
            inc = (E.sem, 1)
        E.queue.append((waits, fn, inc))
        self.ninst += 1 + len(waits)
        for b in reads:
            b.r[tok[0]] = tok
        for b in writes:
            b.w = tok
            b.r = {}
        return tok

    def wait_bufs(self, E, bufs):
        deps = {}
        for b in bufs:
            toks = list(b.r.values()) + ([b.w] if b.w else [])
            for s, v in toks:
                if deps.get(s, 0) < v:
                    deps[s] = v
        waits = [(s, v) for s, v in deps.items() if E.seen.get(s, 0) < v]
        for s, v in waits:
            E.seen[s] = v
        E.queue.append((waits, None, None))

    def emit(self):
        nc = self.nc
        sems = self.sems
        with nc.Block() as block:
            def run(E):
                def body(eng):
                    for waits, fn, inc in E.queue:
                        for s, v in waits:
                            eng.wait_ge(sems[s], v)
                        if fn is not None:
                            ins = fn(eng)
                            ins.then_inc(sems[inc[0]], inc[1])
                return body
            block.tensor(run(self.pe))
            block.scalar(run(self.act))
            block.vector(run(self.dve))
            block.gpsimd(run(self.pool))
            block.sync(run(self.sp))


class TB:
    def __init__(self, t, b):
        self.t = t
        self.b = b


def _blockify(W, blocks):
    din, dout = W.shape
    kc = din // 128
    Wv = W.reshape(kc, 128, dout // 128, 128)
    arr = Wv[:, :, blocks, :]
    return np.ascontiguousarray(arr.transpose(1, 2, 0, 3)).reshape(128, -1)


def _weight_groups():
    g = []
    for f in ("ffn1", "ffn2"):
        pass
    def ffn(name):
        out = []
        for i in range(NJ // 2):
            out.append((name + "_w_in", [2 * i, NJ + 2 * i, 2 * i + 1, NJ + 2 * i + 1], 8))
        for dc in range(8):
            out.append((name + "_w_out", [dc], NJ))
        return out
    g += ffn("ffn1")
    for h in range(NH):
        g.append(("mix_w_in", [32 + h, 40 + h], 8))
        g.append(("mix_w_in", [h, 8 + h, 16 + h, 24 + h], 8))
        g.append(("conv_diag", [h], 0))
    for half in range(2):
        g.append(("hgrn_w_o", [4 * half + i for i in range(4)], 8))
        g.append(("conv_w_o", [4 * half + i for i in range(4)], 8))
        for gg in (2 * half, 2 * half + 1):
            g.append(("mix_w_in", [48 + 2 * gg, 56 + 2 * gg, 48 + 2 * gg + 1, 56 + 2 * gg + 1], 8))
    for half in range(2):
        g.append(("mix_w_out", [4 * half + i for i in range(4)], 8))
    g += ffn("ffn2")
    return g


WGROUPS = _weight_groups()
DIAG_E = CONV_K * 128
WG_E = [(DIAG_E if k == "conv_diag" else len(b) * kc * 128) for (k, b, kc) in WGROUPS]
WG_OFF = []
_o = 0
for (_k, _b, _kc), _e in zip(WGROUPS, WG_E):
    WG_OFF.append(_o)
    if _k != "conv_diag":
        _o += 128 * _e
WALL_N = _o

ADA_NB = 72
PC = {}
_c = 0
for _n, _w in (("c", 8), ("ada_b", 72), ("nf1", 8), ("nmix", 8), ("nf2", 8), ("nfin", 8), ("lb0", 8), ("lb1", 8),
               ("hg", 8), ("cb", 8), ("lng", 8), ("lnb", 8), ("cw", 8 * CONV_K)):
    PC[_n] = (_c, _c + _w)
    _c += _w
NPC = _c


def _fm(v):
    return np.ascontiguousarray(v.reshape(-1, 128).T)


def build(NT=SEQ // T, debug=False, stage=9):
    nc = bass.Bass("TRN2", target_bir_lowering=False)
    fw = FW(nc)
    pe, act, dve, pool, sp = fw.pe, fw.act, fw.dve, fw.pool, fw.sp
    S = NT * T

    xT_d = nc.dram_tensor("xT", [D, S], F32, kind="ExternalInput").ap()
    wall_d = nc.dram_tensor("wall", [WALL_N], F32, kind="ExternalInput").ap()
    ada_d = nc.dram_tensor("ada", [ADA_NB // 2, 128, 2 * 8 * 128], F32, kind="ExternalInput").ap()
    pp_d = nc.dram_tensor("pp", [128, NPC], F32, kind="ExternalInput").ap()
    outT_d = nc.dram_tensor("outT", [D, S], F32, kind="ExternalOutput").ap()
    wscr_d = nc.dram_tensor("wscr", [WALL_N], BF16, kind="Internal").ap()
    cdiag_d = nc.dram_tensor("cdiag", [8, 128, DIAG_E], BF16, kind="Internal").ap()
    dbg_outs = {}

    def sb(name, shape, dt):
        return TB(nc.alloc_sbuf_tensor("sb_" + name, shape, dt), fw.buf(name))

    wslots = [sb(f"wslot{i}", [128, SLOT_E], BF16) for i in range(NSLOT)]
    xs = [sb(f"xs{i}", [128, 8, T], F32) for i in range(2)]
    hT = sb("hT", [128, 8, T], BF16)
    mrg = sb("mrg", [128, 8, T], BF16)
    rs = sb("rs", [128, T], F32)
    ntmp = [sb(f"ntmp{i}", [128, T], F32) for i in range(2)]
    arena = sb("arena", [128, NJ * T // 2], F32)
    gT = arena.t[:, :].bitcast(BF16).rearrange("p (j t) -> p j t", t=T)
    acc = arena.t[:, 0:8 * T].rearrange("p (c t) -> p c t", t=T)
    sa = [sb(f"sa{i}", [128, T], F32) for i in range(2)]
    ogT = sb("ogT", [128, 8, T], BF16)
    u2T = sb("u2T", [128, 8, T], BF16)
    ub = [sb(f"ub{i}", [128, HALO + T], BF16) for i in range(2)]
    ubo = [sb(f"ubo{i}", [128, HALO + T], BF16) for i in range(2)]
    halo = sb("halo", [128, 8, HALO], BF16)
    Sf = sb("Sf", [128, NH, 128], F32)
    Sb0 = sb("Sb0", [128, NH, 128], BF16)
    pp = sb("pp", [128, NPC], F32)
    dv = sb("dv", [128, 160], F32)
    modv = sb("modv", [128, ADA_NB], F32)
    ones = sb("ones", [128, 128], BF16)
    ident = sb("ident", [128, 128], BF16)
    identf = sb("identf", [128, 128], F32)
    smask = sb("smask", [128, T], F32)
    mask2 = sb("mask2", [128, 128], F32)
    adast = [TB(arena.t[:, 2048 * i:2048 * (i + 1)], fw.buf(f"adast{i}")) for i in range(2)]
    NHS = 2
    hs = []
    for i in range(NHS):
        hs.append(dict(
            B1=sb(f"hB1_{i}", [128, T], F32), B2=sb(f"hB2_{i}", [128, T], F32), B3=sb(f"hB3_{i}", [128, T], F32),
            sq=sb(f"hsq_{i}", [128, T], F32), qt=sb(f"hqt_{i}", [128, T], BF16), kt=sb(f"hkt_{i}", [128, T], BF16),
            sog=sb(f"hsog_{i}", [128, T], BF16), Vb=sb(f"hVb_{i}", [128, 4, 128], BF16),
            ktok=sb(f"hktok_{i}", [128, 4, 128], BF16), ktokO=sb(f"hktokO_{i}", [128, 4, 128], BF16), Am=sb(f"hAm_{i}", [128, 4, 128], BF16),
            Sball=sb(f"hSb_{i}", [128, 9, 128], BF16), Usb=sb(f"hUsb_{i}", [128, 8, 128], F32), T1=[sb(f"hT1_{i}_{k}", [128, 128], F32) for k in range(2)],
        ))
        hs[-1]["osq"] = hs[-1]["kt"]
        hs[-1]["ot"] = hs[-1]["B1"]
    mtmp = [sb(f"mtmp{i}", [128, T], F32) for i in range(4)]

    NPS = 7
    psr = [TB(nc.alloc_psum_tensor(f"ps{i}", [128, T], F32), fw.buf(f"ps{i}")) for i in range(NPS)]
    pstr = TB(nc.alloc_psum_tensor("pstr", [128, 4, 128], BF16), fw.buf("pstr"))
    ps_i = [0]

    def nps():
        p = psr[ps_i[0] % NPS]
        ps_i[0] += 1
        return p

    def bl(xs_):
        return [x.b if isinstance(x, TB) else x for x in xs_]

    def ACT(out, in_, func, R, W, **kw):
        fw.op(act, lambda e: e.activation(out=out, in_=in_, func=func, **kw), reads=bl(R), writes=bl(W))

    def TT(E, out, in0, in1, op, R, W):
        fw.op(E, lambda e: e.tensor_tensor(out=out, in0=in0, in1=in1, op=op), reads=bl(R), writes=bl(W))

    def TS(E, out, in0, s1, s2, op0, op1, R, W):
        if s2 is None:
            fw.op(E, lambda e: e.tensor_scalar(out=out, in0=in0, scalar1=s1, scalar2=None, op0=op0), reads=bl(R), writes=bl(W))
        else:
            fw.op(E, lambda e: e.tensor_scalar(out=out, in0=in0, scalar1=s1, scalar2=s2, op0=op0, op1=op1), reads=bl(R), writes=bl(W))

    def STT(E, out, in0, scalar, in1, op0, op1, R, W):
        fw.op(E, lambda e: e.scalar_tensor_tensor(out=out, in0=in0, scalar=scalar, in1=in1, op0=op0, op1=op1), reads=bl(R), writes=bl(W))

    def MM(out, lhsT, rhs, start, stop, R, W):
        fw.op(pe, lambda e: e.matmul(out, lhsT=lhsT, rhs=rhs, start=start, stop=stop), reads=bl(R), writes=bl(W))

    def CP(E, out, in_, R, W):
        fw.op(E, lambda e: e.tensor_copy(out=out, in_=in_), reads=bl(R), writes=bl(W))

    def DMA(E, out, in_, R, W, owner):
        fw.op(E, lambda e: e.dma_start(out=out, in_=in_), reads=bl(R), writes=bl(W), dma=owner.b if isinstance(owner, TB) else owner)

    def MEMSET(E, ap, val, W):
        fw.op(E, lambda e: e.memset(ap, val), writes=bl(W))

    def RSQRT(otb, out, itb, in_, epsname):
        ACT(out, in_, AF.Ln, [itb, dv], [otb], bias=dvc(epsname, 0))
        ACT(out, out, AF.Exp, [otb], [otb], scale=-0.5)

    def dbg(name, tb, ap, shape, dt):
        if not debug:
            return
        d = nc.dram_tensor("dbg_" + name, list(shape), dt, kind="ExternalOutput").ap()
        dbg_outs[name] = fw.buf("dbg_" + name)
        DMA(sp, d, ap, [tb], [dbg_outs[name]], dbg_outs[name])

    def col(name, i=None):
        a, b_ = PC[name]
        if i is None:
            return pp.t[:, a:b_]
        return pp.t[:, a + i:a + i + 1]

    DVC = {}
    _o = [0]

    def dvcol(name, w=8):
        DVC[name] = (_o[0], _o[0] + w)
        _o[0] += w

    for n_ in ("cs", "gm1", "gt1", "gm2", "gm3", "gt3", "nfin", "lb", "oml", "noml", "hg", "lbd", "epsD", "epsH", "eps1", "rmE", "rmO"):
        dvcol(n_)

    def dvc(name, i=None):
        a, b_ = DVC[name]
        if i is None:
            return dv.t[:, a:b_]
        return dv.t[:, a + i:a + i + 1]

    def modc(k, i=None):
        if i is None:
            return modv.t[:, 8 * k:8 * k + 8]
        return modv.t[:, 8 * k + i:8 * k + i + 1]

    DMA(sp, pp.t[:, :], pp_d[:, :], [], [pp], pp)
    wscr_b = fw.buf("wscr")
    wscr_sems = []
    CH = 128 * SLOT_E * 2
    off = 0
    ci = 0
    cast_bufs = []
    import os as _os
    _nocast = bool(int(_os.environ.get('NOCAST', '0')))
    while off < WALL_N and not (_nocast and ci >= 0 + int(_os.environ.get('NCAST', '0'))):
        n = min(CH, WALL_N - off)
        assert n % 2048 == 0
        src = wall_d[off:off + n].rearrange("(r c) -> r c", c=2048)
        dst = wscr_d[off:off + n].rearrange("(r c) -> r c", c=2048)
        cb_ = fw.buf(f"cast{ci % 4}") if ci < 4 else cast_bufs[ci % 4]
        if ci < 4:
            cast_bufs.append(cb_)
        fw.op(pool, (lambda s_, d_: (lambda e: e.dma_start(out=d_, in_=s_)))(src, dst), reads=[], writes=[cb_], dma=cb_)
        off += n
        ci += 1

    MEMSET(pool, ones.t[:, :], 1.0, [ones])
    MEMSET(pool, identf.t[:, :], 1.0, [identf])
    fw.op(pool, lambda e: e.affine_select(out=identf.t[:, :], in_=identf.t[:, :], pattern=[[-1, 128]], compare_op=ALU.is_equal,
                                          fill=0.0, base=0, channel_multiplier=1), reads=[identf.b], writes=[identf.b])
    CP(dve, ident.t[:, :], identf.t[:, :], [identf], [ident])
    MEMSET(pool, mask2.t[:, :], 1.0, [mask2])
    fw.op(pool, lambda e: e.affine_select(out=mask2.t[:, :], in_=mask2.t[:, :], pattern=[[1, 128]], compare_op=ALU.is_ge,
                                          fill=0.0, base=0, channel_multiplier=-1), reads=[mask2.b], writes=[mask2.b])
    MEMSET(pool, mask2.t[0:64, 64:128], 0.0, [mask2])
    MEMSET(pool, dvc("rmE")[0:64, :], 1.0, [dv])
    MEMSET(pool, dvc("rmE")[64:128, :], 0.0, [dv])
    MEMSET(pool, dvc("rmO")[0:64, :], 0.0, [dv])
    MEMSET(pool, dvc("rmO")[64:128, :], 1.0, [dv])
    MEMSET(pool, smask.t[:, :], 1.0, [smask])
    MEMSET(pool, smask.t[:, :].rearrange("p (c t) -> p c t", t=64)[:, :, 0:1], 0.0, [smask])
    MEMSET(pool, Sf.t[:, :, :], 0.0, [Sf])
    MEMSET(pool, Sb0.t[:, :, :], 0.0, [Sb0])
    MEMSET(pool, halo.t[:, :, :], 0.0, [halo])

    cdg_b = fw.buf("cdiag")
    a0_cw, _ = PC["cw"]
    for cbk in range(8):
        stg = wslots[cbk % 2]
        stv_ = stg.t[:, 0:DIAG_E].rearrange("p (j c) -> p j c", c=128)
        for j in range(CONV_K):
            TS(dve, stv_[:, j, :], identf.t[:, :], pp.t[:, a0_cw + cbk * CONV_K + j:a0_cw + cbk * CONV_K + j + 1], None,
               ALU.mult, None, [identf, pp], [stg])
        DMA(sp, cdiag_d[cbk], stg.t[:, 0:DIAG_E], [stg], [cdg_b], stg)
    MEMSET(pool, dvc("epsD"), float(D * EPS), [dv])
    MEMSET(pool, dvc("epsH"), float(128 * EPS), [dv])
    MEMSET(pool, dvc("eps1"), float(EPS), [dv])
    ACT(dvc("cs"), col("c"), AF.Silu, [pp], [dv])
    psmod = nps()
    for g2 in range(ADA_NB // 2):
        st = adast[g2 % 2]
        DMA(sp, st.t, ada_d[g2], [], [st], st)
        stv = st.t.rearrange("p (b k c) -> p b k c", b=2, k=8)
        for b_ in range(2):
            j = 2 * g2 + b_
            for kc in range(8):
                MM(psmod.t[:, j:j + 1], stv[:, b_, kc, :], dvc("cs", kc), kc == 0, kc == 7, [st, dv], [psmod])
    TT(dve, modv.t[:, :], psmod.t[:, 0:ADA_NB], col("ada_b"), ALU.add, [psmod, pp], [modv])
    for (gm, nf, sc) in (("gm1", "nf1", 1), ("gm2", "nmix", 4), ("gm3", "nf2", 7)):
        TS(dve, dvc(gm), modc(sc), 1.0, 32.0, ALU.add, ALU.mult, [modv], [dv])
        TT(dve, dvc(gm), dvc(gm), col(nf), ALU.mult, [dv, pp], [dv])
    TS(dve, dvc("gt1"), modc(2), 0.5, None, ALU.mult, None, [modv], [dv])
    TS(dve, dvc("gt3"), modc(8), 0.5, None, ALU.mult, None, [modv], [dv])
    TS(dve, dvc("nfin"), col("nfin"), 32.0, None, ALU.mult, None, [pp], [dv])
    TT(dve, dvc("lbd"), col("lb0"), col("lb1"), ALU.subtract, [pp], [dv])
    ACT(dvc("lb"), dvc("lbd"), AF.Sigmoid, [dv], [dv])
    TS(dve, dvc("oml"), dvc("lb"), -1.0, 1.0, ALU.mult, ALU.add, [dv], [dv])
    TS(dve, dvc("noml"), dvc("oml"), -1.0, None, ALU.mult, None, [dv], [dv])
    TS(dve, dvc("hg"), col("hg"), float(np.sqrt(128.0)), None, ALU.mult, None, [pp], [dv])
    if debug:
        dbg("modv", modv, modv.t[:, :], [128, ADA_NB], F32)

    NG = len(WGROUPS)
    wstate = dict(next_load=0, next_use=0)
    total_groups = NT * NG

    def issue_load():
        n = wstate["next_load"]
        if n >= total_groups:
            return
        g = n % NG
        slot = wslots[n % NSLOT]
        E = WG_E[g]
        if WGROUPS[g][0] == "conv_diag":
            DMA(sp, slot.t[:, 0:E], cdiag_d[WGROUPS[g][1][0]], [cdg_b], [slot], slot)
        else:
            src = wscr_d[WG_OFF[g]:WG_OFF[g] + 128 * E].rearrange("(p e) -> p e", p=128)
            DMA(sp, slot.t[:, 0:E], src, cast_bufs, [slot], slot)
        wstate["next_load"] = n + 1

    def next_w(expect_key):
        n = wstate["next_use"]
        g = n % NG
        key, blocks, kc = WGROUPS[g]
        assert key == expect_key, (key, expect_key)
        slot = wslots[n % NSLOT]
        wstate["next_use"] = n + 1
        if key == "conv_diag":
            v = slot.t[:, 0:WG_E[g]].rearrange("p (j c) -> p j c", c=128)
        else:
            v = slot.t[:, 0:WG_E[g]].rearrange("p (b k c) -> p b k c", b=len(blocks), k=kc)
        return slot, v

    def done_w():
        issue_load()

    def rms_stats(x):
        ACT(mrg.t[:, :, :], x.t[:, :, :], AF.Square, [x], [mrg])
        p = nps()
        for dc in range(8):
            MM(p.t[:, :], ones.t[:, :], mrg.t[:, dc, :], dc == 0, dc == 7, [ones, mrg], [p])
        RSQRT(rs, rs.t[:, :], p, p.t[:, :], "epsD")

    def norm_mod(x, gm, shk):
        rms_stats(x)
        for dc in range(8):
            tmp = ntmp[dc % 2]
            TT(dve, tmp.t[:, :], x.t[:, dc, :], rs.t[:, :], ALU.mult, [x, rs], [tmp])
            ACT(hT.t[:, dc, :], tmp.t[:, :], AF.Identity, [tmp, dv, modv], [hT], scale=dvc(gm, dc), bias=modc(shk, dc))

    def ffn(x, name, gm, shk, gt):
        norm_mod(x, gm, shk)
        for g in range(NJ // 2):
            slot, wv = next_w(name + "_w_in")
            for jj in range(2):
                j = 2 * g + jj
                pa = nps()
                for kc in range(8):
                    MM(pa.t[:, :], wv[:, 2 * jj, kc, :], hT.t[:, kc, :], kc == 0, kc == 7, [slot, hT], [pa])
                pb = nps()
                for kc in range(8):
                    MM(pb.t[:, :], wv[:, 2 * jj + 1, kc, :], hT.t[:, kc, :], kc == 0, kc == 7, [slot, hT], [pb])
                s_ = sa[j % 2]
                ACT(s_.t[:, :], pa.t[:, :], AF.Silu, [pa], [s_])
                TT(dve, gT[:, j, :], pb.t[:, :], s_.t[:, :], ALU.mult, [pb, s_], [arena])
            done_w()
        for dc in range(8):
            slot, wv = next_w(name + "_w_out")
            po = nps()
            for j in range(NJ):
                MM(po.t[:, :], wv[:, 0, j, :], gT[:, j, :], j == 0, j == NJ - 1, [slot, arena], [po])
            done_w()
            STT(dve, x.t[:, dc, :], po.t[:, :], dvc(gt, dc), x.t[:, dc, :], ALU.mult, ALU.add, [po, dv, x], [x])

    def hgrn_head_proj(h, H):
        slot, wv = next_w("mix_w_in")
        pq = nps()
        for kc in range(8):
            MM(pq.t[:, :], wv[:, 0, kc, :], hT.t[:, kc, :], kc == 0, kc == 7, [slot, hT], [pq])
        pf = nps()
        for kc in range(8):
            MM(pf.t[:, :], wv[:, 1, kc, :], hT.t[:, kc, :], kc == 0, kc == 7, [slot, hT], [pf])
        pg = nps()
        for kc in range(8):
            MM(pg.t[:, :], wv[:, 3, kc, :], hT.t[:, kc, :], kc == 0, kc == 7, [slot, hT], [pg])
        pv = nps()
        pvv = pv.t[:, :].rearrange("p (b c) -> p b c", c=128)
        for tb in range(4):
            for kc in range(8):
                MM(pvv[:, tb, :], hT.t[:, kc, tb * 128:(tb + 1) * 128], wv[:, 2, kc, :], kc == 0, kc == 7, [slot, hT], [pv])
        done_w()
        return pq, pf, pg, pv, pvv

    def hgrn_head_elem(h, H, pq, pf, pg, pv, pvv):
        B1, B2, B3 = H["B1"], H["B2"], H["B3"]
        ACT(H["sq"].t[:, :], pq.t[:, :], AF.Silu, [pq], [H["sq"]])
        ACT(H["sog"].t[:, :], pg.t[:, :], AF.Silu, [pg], [H["sog"]])
        ACT(B1.t[:, :], pf.t[:, :], AF.Sigmoid, [pf], [B1])
        CP(dve, H["Vb"].t[:, :, :], pvv, [pv], [H["Vb"]])
        ACT(B2.t[:, :], B1.t[:, :], AF.Ln, [B1, dv], [B2], scale=dvc("oml", h), bias=dvc("lb", h))
        fw.op(dve, lambda e: e.tensor_tensor_scan(out=B3.t[:, :], data0=smask.t[:, :], data1=B2.t[:, :], initial=0.0,
                                                  op0=ALU.mult, op1=ALU.add), reads=[smask.b, B2.b], writes=[B3.b])
        TS(dve, B1.t[:, :], B1.t[:, :], dvc("noml", h), dvc("oml", h), ALU.mult, ALU.add, [B1, dv], [B1])
        ACT(B2.t[:, :], B3.t[:, :], AF.Exp, [B3], [B2])
        ACT(B3.t[:, :], B3.t[:, :], AF.Exp, [B3], [B3], scale=-1.0)
        STT(dve, H["qt"].t[:, :], H["sq"].t[:, :], float(128.0 ** -0.5), B2.t[:, :], ALU.mult, ALU.mult, [H["sq"], B2], [H["qt"]])
        TT(dve, H["kt"].t[:, :], B1.t[:, :], B3.t[:, :], ALU.mult, [B1, B3], [H["kt"]])

    def hgrn_head_rest(h, H):
        B2 = H["B2"]
        for tb in range(4):
            fw.op(pe, (lambda tb_: (lambda e: e.transpose(out=pstr.t[:, tb_, :], in_=H["kt"].t[:, tb_ * 128:(tb_ + 1) * 128],
                                                          identity=ident.t[:, :])))(tb),
                  reads=[H["kt"].b, ident.b], writes=[pstr.b])
        TS(dve, H["ktok"].t[:, :, :], pstr.t[:, :, :], dvc("rmE", 0), None, ALU.mult, None, [pstr, dv], [H["ktok"]])
        TS(dve, H["ktokO"].t[:, :, :], pstr.t[:, :, :], dvc("rmO", 0), None, ALU.mult, None, [pstr, dv], [H["ktokO"]])
        pa = nps()
        pav = pa.t[:, :].rearrange("p (b c) -> p b c", c=128)
        for tb in range(4):
            sl = slice(tb * 128, (tb + 1) * 128)
            MM(pav[:, tb, :], H["kt"].t[:, sl], H["qt"].t[:, sl], True, True, [H["kt"], H["qt"]], [pa])
        TT(dve, H["Am"].t[:, :, :], pav, mask2.t[:, :].unsqueeze(1).to_broadcast([128, 4, 128]), ALU.mult, [pa, mask2], [H["Am"]])
        pus = [nps(), nps()]
        for c in range(8):
            tb, pb_ = c // 2, (c % 2) * 64
            pu = pus[c // 4]
            kk = H["ktok"] if c % 2 == 0 else H["ktokO"]
            MM(pu.t[:, (c % 4) * 128:(c % 4 + 1) * 128], kk.t[:, tb, :], H["Vb"].t[:, tb, :],
               True, True, [kk, H["Vb"]], [pu])
        Usb = H["Usb"]
        CP(dve, Usb.t[:, 0:4, :], pus[0].t[:, :].rearrange("p (b c) -> p b c", c=128), [pus[0]], [Usb])
        ACT(Usb.t[:, 4:8, :], pus[1].t[:, :].rearrange("p (b c) -> p b c", c=128), AF.Copy, [pus[1]], [Usb])
        Sball = H["Sball"]
        for c in range(8):
            T1 = H["T1"][c % 2]
            TT(dve, T1.t[:, :], Usb.t[:, c, :], Sf.t[:, h, :], ALU.add, [Usb, Sf], [T1])
            ebl = B2.t[:, c * 64 + 63:c * 64 + 64]
            ACT(Sf.t[:, h, :], T1.t[:, :], AF.Identity, [T1, B2], [Sf], scale=ebl)
            TS(dve, Sball.t[:, c + 1, :], T1.t[:, :], ebl, None, ALU.mult, None, [T1, B2], [Sball])
        return pus

    _alt = [0]

    def act_or_dve():
        return dve

    def hgrn_head_back(h, H, pus):
        Sball = H["Sball"]
        po = nps()
        for c in range(8):
            tb, pb_ = c // 2, (c % 2) * 64
            cs_ = slice(c * 64, (c + 1) * 64)
            if c == 0:
                MM(po.t[:, cs_], Sb0.t[:, h, :], H["qt"].t[:, cs_], True, False, [Sb0, H["qt"]], [po])
            else:
                MM(po.t[:, cs_], Sball.t[:, c, :], H["qt"].t[:, cs_], True, False, [Sball, H["qt"]], [po])
            MM(po.t[:, cs_], H["Vb"].t[:, tb, :], H["Am"].t[:, tb, pb_:pb_ + 64], False, True,
               [H["Vb"], H["Am"]], [po])
        CP(dve, Sb0.t[:, h, :], Sball.t[:, 8, :], [Sball], [Sb0])
        ACT(H["osq"].t[:, :], po.t[:, :], AF.Square, [po], [H["osq"]])
        pn = nps()
        MM(pn.t[:, :], ones.t[:, :], H["osq"].t[:, :], True, True, [ones, H["osq"]], [pn])
        RSQRT(H["ot"], H["ot"].t[:, :], pn, pn.t[:, :], "epsH")
        TT(dve, H["ot"].t[:, :], po.t[:, :], H["ot"].t[:, :], ALU.mult, [po, H["ot"]], [H["ot"]])
        STT(dve, ogT.t[:, h, :], H["ot"].t[:, :], dvc("hg", h), H["sog"].t[:, :], ALU.mult, ALU.mult, [H["ot"], dv, H["sog"]], [ogT])

    def conv_proj(cbk):
        slot, wv = next_w("mix_w_in")
        pa = nps()
        for kc in range(8):
            MM(pa.t[:, :], wv[:, 0, kc, :], hT.t[:, kc, :], kc == 0, kc == 7, [slot, hT], [pa])
        pb = nps()
        for kc in range(8):
            MM(pb.t[:, :], wv[:, 1, kc, :], hT.t[:, kc, :], kc == 0, kc == 7, [slot, hT], [pb])
        done_w()
        s_ = sa[cbk % 2]
        U = ub[cbk % 2]
        ACT(s_.t[:, :], pb.t[:, :], AF.Sigmoid, [pb], [s_])
        ACT(U.t[:, 0:HALO], halo.t[:, cbk, :], AF.Copy, [halo], [U])
        TT(dve, U.t[:, HALO:HALO + T], pa.t[:, :], s_.t[:, :], ALU.mult, [pa, s_], [U])
        ACT(halo.t[:, cbk, :], U.t[:, T:T + HALO], AF.Copy, [U], [halo])
        Uo = ubo[cbk % 2]
        ACT(Uo.t[:, 0:HALO + T - 1], U.t[:, 1:HALO + T], AF.Copy, [U], [Uo])

    def conv_taps(cbk):
        slot, dg = next_w("conv_diag")
        U = ub[cbk % 2]
        pc = nps()
        Uo = ubo[cbk % 2]
        for j in range(CONV_K):
            if j % 2 == 0:
                MM(pc.t[:, :], dg[:, j, :], U.t[:, j:j + T], j == 0, j == CONV_K - 1, [slot, U], [pc])
            else:
                MM(pc.t[:, :], dg[:, j, :], Uo.t[:, j - 1:j - 1 + T], j == 0, j == CONV_K - 1, [slot, Uo], [pc])
        done_w()
        ACT(acc[:, cbk, :], pc.t[:, :], AF.Identity, [pc, pp], [arena], bias=col("cb", cbk))

    def conv_ln():
        ACT(u2T.t[:, :, :], acc, AF.Copy, [arena], [u2T])
        ACT(mrg.t[:, :, :], acc, AF.Square, [arena], [mrg])
        pm = nps()
        for cbk in range(8):
            MM(pm.t[:, :], ones.t[:, :], u2T.t[:, cbk, :], cbk == 0, cbk == 7, [ones, u2T], [pm])
        pq2 = nps()
        for cbk in range(8):
            MM(pq2.t[:, :], ones.t[:, :], mrg.t[:, cbk, :], cbk == 0, cbk == 7, [ones, mrg], [pq2])
        M1, M2 = mtmp[0], mtmp[1]
        TS(dve, M1.t[:, :], pm.t[:, :], 1.0 / D, None, ALU.mult, None, [pm], [M1])
        TT(dve, M2.t[:, :], M1.t[:, :], M1.t[:, :], ALU.mult, [M1], [M2])
        STT(dve, M2.t[:, :], pq2.t[:, :], 1.0 / D, M2.t[:, :], ALU.mult, ALU.subtract, [pq2, M2], [M2])
        RSQRT(M2, M2.t[:, :], M2, M2.t[:, :], "eps1")
        STT(dve, M1.t[:, :], M1.t[:, :], -1.0, M2.t[:, :], ALU.mult, ALU.mult, [M1, M2], [M1])
        for cbk in range(8):
            t1 = mtmp[2 + cbk % 2]
            TT(dve, t1.t[:, :], acc[:, cbk, :], M2.t[:, :], ALU.mult, [arena, M2], [t1])
            TT(dve, t1.t[:, :], t1.t[:, :], M1.t[:, :], ALU.add, [t1, M1], [t1])
            ACT(u2T.t[:, cbk, :], t1.t[:, :], AF.Silu, [t1, pp], [u2T], scale=col("lng", cbk), bias=col("lnb", cbk))

    def merge_and_out(x):
        for half in range(2):
            sA, wA = next_w("hgrn_w_o")
            sB, wB = next_w("conv_w_o")
            sG0, wG0 = next_w("mix_w_in")
            sG1, wG1 = next_w("mix_w_in")
            for i in range(4):
                dc = 4 * half + i
                sG, wG = (sG0, wG0) if i < 2 else (sG1, wG1)
                ii = i % 2
                pya = nps()
                for kc in range(8):
                    MM(pya.t[:, :], wA[:, i, kc, :], ogT.t[:, kc, :], kc == 0, kc == 7, [sA, ogT], [pya])
                pyb = nps()
                for kc in range(8):
                    MM(pyb.t[:, :], wB[:, i, kc, :], u2T.t[:, kc, :], kc == 0, kc == 7, [sB, u2T], [pyb])
                pga = nps()
                for kc in range(8):
                    MM(pga.t[:, :], wG[:, 2 * ii, kc, :], hT.t[:, kc, :], kc == 0, kc == 7, [sG, hT], [pga])
                pgb = nps()
                for kc in range(8):
                    MM(pgb.t[:, :], wG[:, 2 * ii + 1, kc, :], hT.t[:, kc, :], kc == 0, kc == 7, [sG, hT], [pgb])
                g1, g2_, m1, m2 = mtmp[0], mtmp[1], mtmp[2], mtmp[3]
                ACT(g1.t[:, :], pga.t[:, :], AF.Sigmoid, [pga], [g1])
                ACT(g2_.t[:, :], pgb.t[:, :], AF.Sigmoid, [pgb], [g2_])
                TT(dve, m1.t[:, :], pya.t[:, :], g1.t[:, :], ALU.mult, [pya, g1], [m1])
                TT(dve, m2.t[:, :], pyb.t[:, :], g2_.t[:, :], ALU.mult, [pyb, g2_], [m2])
                TT(dve, mrg.t[:, dc, :], m1.t[:, :], m2.t[:, :], ALU.add, [m1, m2], [mrg])
            for _ in range(4):
                done_w()
        for half in range(2):
            slot, wv = next_w("mix_w_out")
            for i in range(4):
                dc = 4 * half + i
                po = nps()
                for kc in range(8):
                    MM(po.t[:, :], wv[:, i, kc, :], mrg.t[:, kc, :], kc == 0, kc == 7, [slot, mrg], [po])
                STT(dve, x.t[:, dc, :], po.t[:, :], modc(5, dc), x.t[:, dc, :], ALU.mult, ALU.add, [po, modv, x], [x])
            done_w()

    def mixer(x, ti):
        norm_mod(x, "gm2", 3)
        prev = None
        for h in range(NH):
            H = hs[h % NHS]
            conv_proj(h)
            pp_ = hgrn_head_proj(h, H)
            hgrn_head_elem(h, H, *pp_)
            conv_taps(h)
            pus = hgrn_head_rest(h, H)
            if prev is not None:
                hgrn_head_back(*prev)
            prev = (h, H, pus)
        hgrn_head_back(*prev)
        if debug and ti == 0:
            dbg("ogT", ogT, ogT.t[:, :, :], [128, 8, T], BF16)
        conv_ln()
        if debug and ti == 0:
            dbg("u2T", u2T, u2T.t[:, :, :], [128, 8, T], BF16)
        merge_and_out(x)

    xv = xT_d.rearrange("(dc p) t -> p dc t", p=128)
    ov = outT_d.rearrange("(dc p) t -> p dc t", p=128)
    if stage >= 1:
        for _ in range(NSLOT):
            issue_load()
    DMA(sp, xs[0].t[:, :, :], xv[:, :, 0:T], [], [xs[0]], xs[0])
    for ti in range(NT if stage >= 1 else 0):
        fw.new_epoch()
        x = xs[ti % 2]
        if ti + 1 < NT:
            xn = xs[(ti + 1) % 2]
            DMA(sp, xn.t[:, :, :], xv[:, :, (ti + 1) * T:(ti + 2) * T], [], [xn], xn)
        ffn(x, "ffn1", "gm1", 0, "gt1")
        if debug and ti == 0:
            dbg("x1", x, x.t[:, :, :], [128, 8, T], F32)
        if stage < 2:
            DMA(sp, ov[:, :, ti * T:(ti + 1) * T], x.t[:, :, :], [x], [], x)
            continue
        mixer(x, ti)
        if debug and ti == 0:
            dbg("x2", x, x.t[:, :, :], [128, 8, T], F32)
        if stage < 3:
            DMA(sp, ov[:, :, ti * T:(ti + 1) * T], x.t[:, :, :], [x], [], x)
            continue
        ffn(x, "ffn2", "gm3", 6, "gt3")
        rms_stats(x)
        for dc in range(8):
            STT(dve, x.t[:, dc, :], x.t[:, dc, :], dvc("nfin", dc), rs.t[:, :], ALU.mult, ALU.mult, [x, dv, rs], [x])
        DMA(sp, ov[:, :, ti * T:(ti + 1) * T], x.t[:, :, :], [x], [], x)
    fw.wait_bufs(sp, [xs[0].b, xs[1].b] + list(dbg_outs.values()) + cast_bufs)
    fw.emit()
    return nc, fw


def prep_shared(inp):
    W = {
        "ffn1_w_in": inp["ffn1_w_in"][0], "ffn1_w_out": inp["ffn1_w_out"][0],
        "ffn2_w_in": inp["ffn2_w_in"][0], "ffn2_w_out": inp["ffn2_w_out"][0],
        "mix_w_in": inp["mix_w_in"][0], "hgrn_w_o": inp["hgrn_w_o"][0],
        "conv_w_o": inp["conv_w_o"][0], "mix_w_out": inp["mix_w_out"][0],
    }
    wall = np.empty(WALL_N, np.float32)
    for g, (key, blocks, kc) in enumerate(WGROUPS):
        if key == "conv_diag":
            continue
        wall[WG_OFF[g]:WG_OFF[g] + 128 * WG_E[g]] = _blockify(W[key], blocks).reshape(-1)
    ada = np.stack([_blockify(inp["ada_w"][0], [2 * g, 2 * g + 1]) for g in range(ADA_NB // 2)])
    return wall, ada


def prep_pp(inp, b):
    pp = np.zeros((128, NPC), np.float32)

    def put(name, arr):
        a, b_ = PC[name]
        pp[:, a:b_] = arr

    put("c", _fm(inp["c"][b]))
    put("ada_b", _fm(inp["ada_b"][0]))
    put("nf1", _fm(inp["norm_ffn1"][0]))
    put("nmix", _fm(inp["norm_mix"][0]))
    put("nf2", _fm(inp["norm_ffn2"][0]))
    put("nfin", _fm(inp["norm_final"]))
    put("lb0", _fm(inp["hgrn_lb"][0]))
    put("lb1", _fm(inp["hgrn_lb"][1]))
    put("hg", _fm(inp["hgrn_g"][0]))
    put("cb", _fm(inp["conv_b"][0]))
    put("lng", _fm(inp["conv_ln_g"][0]))
    put("lnb", _fm(inp["conv_ln_b"][0]))
    cw = inp["conv_w"][0]
    put("cw", np.ascontiguousarray(cw.reshape(CONV_K, 8, 128).transpose(2, 1, 0)).reshape(128, 8 * CONV_K))
    return pp


_CACHE = {}


def kernel(**inputs):
    inp = {k: np.asarray(v) for k, v in inputs.items()}
    x = inp["x"]
    B, S, _ = x.shape
    NT = S // T
    if NT not in _CACHE:
        _CACHE[NT] = build(NT)[0]
    nc = _CACHE[NT]
    wall, ada = prep_shared(inp)
    in_maps = []
    for b in range(B):
        in_maps.append({
            "xT": np.ascontiguousarray(x[b].T),
            "wall": wall,
            "ada": ada,
            "pp": prep_pp(inp, b),
        })
    res = run_bass_kernel_spmd(nc, in_maps, core_ids=list(range(B)))
    out = np.stack([np.ascontiguousarray(r["outT"].T) for r in res.results], axis=0)
    return out.astype(np.float32, copy=False)
```
